# Optimizing a Trainium2 kernel written in Bass

```python
import jax, jax.numpy as jnp
from jax import lax
import numpy as np

D_MODEL = 1024
BATCH = 4
SEQ = 4096
DEPTH = 4

PLE_DIM = 256
D_MIX = D_MODEL
W_GRP = D_MIX // 4
N_HEADS_GRP = 4
HEAD_DIM = W_GRP // N_HEADS_GRP
GMLP_CHUNK = 128
RGLRU_CONV = 4
RGLRU_C = 8.0
HGRN_CHUNK = 64
POOL_WINDOWS = (2, 4, 8, 16)
D_FF = 2816
FFN_CONV = 3
EPS = 1e-6
COLS_A = 2 * W_GRP
COLS_B = 2 * W_GRP
COLS_C = 4 * W_GRP
COLS_D = W_GRP
OFF_B = COLS_A
OFF_C = OFF_B + COLS_B
OFF_D = OFF_C + COLS_C
D_PROJ = OFF_D + COLS_D

kernel_name = "hymba_style_gmlp_rglru_hgrn2_pool_hybrid"


def rms_norm(x, g):
    xf = x.astype(jnp.float32)
    y = xf * lax.rsqrt(jnp.mean(xf * xf, axis=-1, keepdims=True) + EPS)
    return (y * g.astype(jnp.float32)).astype(x.dtype)


def causal_dwconv(x, w, b):
    k_width = w.shape[0]
    s = x.shape[1]
    xp = jnp.pad(x, ((0, 0), (k_width - 1, 0), (0, 0)))
    y = b
    for k in range(k_width):
        y = y + xp[:, k:k + s] * w[k]
    return y


def gmlp_mixer(ab, ln_g, ln_b, ws, bs):
    bsz, s, _ = ab.shape
    ab = jax.nn.gelu(ab)
    u, v = jnp.split(ab, 2, axis=-1)
    vf = v.astype(jnp.float32)
    mu = jnp.mean(vf, axis=-1, keepdims=True)
    var = jnp.mean(jnp.square(vf - mu), axis=-1, keepdims=True)
    vn = ((vf - mu) * lax.rsqrt(var + EPS) * ln_g.astype(jnp.float32) + ln_b.astype(jnp.float32)).astype(v.dtype)
    vn = vn.reshape(bsz, s // GMLP_CHUNK, GMLP_CHUNK, N_HEADS_GRP, HEAD_DIM)
    mask = jnp.tril(jnp.ones((GMLP_CHUNK, GMLP_CHUNK), dtype=bool))
    wm = jnp.where(mask, ws, jnp.zeros_like(ws))
    sv = jnp.einsum('hts,bnshd->bnthd', wm, vn) + bs.T[:, :, None]
    return u * sv.reshape(bsz, s, W_GRP)


def rglru_mixer(xb, gb, conv_w, conv_b, wa, ba, wx, bx, lam):
    bsz, s, _ = xb.shape
    xc = causal_dwconv(xb, conv_w, conv_b)
    xh = xc.reshape(bsz, s, N_HEADS_GRP, HEAD_DIM)
    r = jax.nn.sigmoid(jnp.einsum('bshd,hde->bshe', xh, wa).reshape(bsz, s, W_GRP) + ba)
    i = jax.nn.sigmoid(jnp.einsum('bshd,hde->bshe', xh, wx).reshape(bsz, s, W_GRP) + bx)
    log_a = -RGLRU_C * r.astype(jnp.float32) * jax.nn.softplus(-lam.astype(jnp.float32))
    a = jnp.exp(log_a)
    mult = jnp.sqrt(-jnp.expm1(2.0 * log_a))
    bterm = mult * (i * xc).astype(jnp.float32)

    def combine(c1, c2):
        a1, b1 = c1
        a2, b2 = c2
        return a1 * a2, a2 * b1 + b2

    _, h = lax.associative_scan(combine, (a, bterm), axis=1)
    return h.astype(xb.dtype) * jax.nn.gelu(gb)


def hgrn2_mixer(q, f, i, g, lb, norm_g):
    bsz, s, _ = q.shape
    n_chunks = s // HGRN_CHUNK
    qf = jax.nn.silu(q.astype(jnp.float32))
    fgate = lb + (1.0 - lb) * jax.nn.sigmoid(f.astype(jnp.float32))
    log_f = jnp.log(fgate)
    kf = 1.0 - fgate
    vf = i.astype(jnp.float32)

    def to_chunks(t):
        return t.reshape(bsz, n_chunks, HGRN_CHUNK, N_HEADS_GRP, HEAD_DIM).transpose(1, 0, 3, 2, 4)

    qc, kc, vc = to_chunks(qf), to_chunks(kf), to_chunks(vf)
    bc = jnp.cumsum(to_chunks(log_f), axis=3)
    mask = jnp.tril(jnp.ones((HGRN_CHUNK, HGRN_CHUNK), dtype=bool))[:, :, None]

    def step(state, xs):
        qq, kk, vv, bb = xs
        diff = bb[:, :, :, None, :] - bb[:, :, None, :, :]
        decay = jnp.exp(jnp.where(mask, diff, -jnp.inf))
        att = jnp.einsum('bhtd,bhsd,bhtsd->bhts', qq, kk, decay)
        o = jnp.einsum('bhts,bhsv->bhtv', att, vv) + jnp.einsum('bhtd,bhdv->bhtv', qq * jnp.exp(bb), state)
        bl = bb[:, :, -1:, :]
        new_state = jnp.exp(bl[:, :, 0, :])[..., None] * state + jnp.einsum('bhsd,bhsv->bhdv', kk * jnp.exp(bl - bb), vv)
        return new_state, o

    s0 = jnp.zeros((bsz, N_HEADS_GRP, HEAD_DIM, HEAD_DIM), jnp.float32)
    _, o = lax.scan(step, s0, (qc, kc, vc, bc))
    o = o.transpose(1, 0, 3, 2, 4).reshape(bsz, s, N_HEADS_GRP, HEAD_DIM)
    o = o * lax.rsqrt(jnp.mean(o * o, axis=-1, keepdims=True) + EPS) * norm_g.astype(jnp.float32)
    o = o.reshape(bsz, s, W_GRP) * jax.nn.silu(g.astype(jnp.float32))
    return o.astype(q.dtype)


def pool_mixer(xd, wd, scale):
    bsz, s, _ = xd.shape
    xf = xd.astype(jnp.float32)
    cs = jnp.cumsum(xf, axis=1)
    pos = jnp.arange(1, s + 1, dtype=jnp.float32)[None, :, None]
    outs = []
    for j, w in enumerate(POOL_WINDOWS):
        c = cs[..., j * HEAD_DIM:(j + 1) * HEAD_DIM]
        shifted = jnp.pad(c, ((0, 0), (w, 0), (0, 0)))[:, :s]
        mean = (c - shifted) / jnp.minimum(pos, float(w))
        outs.append(mean - xf[..., j * HEAD_DIM:(j + 1) * HEAD_DIM])
    pooled = jnp.stack(outs, axis=2)
    y = jnp.einsum('bsgd,gde->bsge', pooled, wd.astype(jnp.float32)).reshape(bsz, s, W_GRP)
    return (y * scale.astype(jnp.float32)).astype(xd.dtype)


def setup_inputs(seed: int = 0) -> dict:
    key = jax.random.key(seed)
    ks = jax.random.split(key, 32)

    def nrm(k, shape, scale):
        return jax.random.normal(k, shape, jnp.float32) * scale

    u = jax.random.uniform(ks[14], (DEPTH, W_GRP), jnp.float32, 0.9, 0.999)
    a_base = u ** (1.0 / RGLRU_C)
    b_lam = jnp.log(a_base) - jnp.log1p(-a_base)
    return {
        "x": nrm(ks[0], (BATCH, SEQ, D_MODEL), 1.0),
        "p": nrm(ks[1], (DEPTH, BATCH, SEQ, PLE_DIM), 1.0),
        "norm1_g": 1.0 + nrm(ks[2], (DEPTH, D_MODEL), 0.02),
        "w_in": nrm(ks[3], (DEPTH, D_MODEL, D_PROJ), D_MODEL ** -0.5),
        "a_ln_g": 1.0 + nrm(ks[4], (DEPTH, W_GRP), 0.02),
        "a_ln_b": nrm(ks[5], (DEPTH, W_GRP), 0.02),
        "a_ws": nrm(ks[6], (DEPTH, N_HEADS_GRP, GMLP_CHUNK, GMLP_CHUNK), GMLP_CHUNK ** -0.5),
        "a_bs": 1.0 + nrm(ks[7], (DEPTH, N_HEADS_GRP, GMLP_CHUNK), 0.1),
        "b_conv_w": nrm(ks[8], (DEPTH, RGLRU_CONV, W_GRP), RGLRU_CONV ** -0.5),
        "b_conv_b": nrm(ks[9], (DEPTH, W_GRP), 0.02),
        "b_wa": nrm(ks[10], (DEPTH, N_HEADS_GRP, HEAD_DIM, HEAD_DIM), HEAD_DIM ** -0.5),
        "b_ba": nrm(ks[11], (DEPTH, W_GRP), 0.02),
        "b_wx": nrm(ks[12], (DEPTH, N_HEADS_GRP, HEAD_DIM, HEAD_DIM), HEAD_DIM ** -0.5),
        "b_bx": nrm(ks[13], (DEPTH, W_GRP), 0.02),
        "b_lam": b_lam,
        "c_lb": nrm(ks[15], (DEPTH, W_GRP), 0.5),
        "c_norm_g": 1.0 + nrm(ks[16], (DEPTH, HEAD_DIM), 0.02),
        "d_w": nrm(ks[17], (DEPTH, N_HEADS_GRP, HEAD_DIM, HEAD_DIM), HEAD_DIM ** -0.5),
        "d_scale": 1.0 + nrm(ks[18], (DEPTH, W_GRP), 0.1),
        "w_out": nrm(ks[19], (DEPTH, D_MIX, D_MODEL), D_MIX ** -0.5),
        "norm2_g": 1.0 + nrm(ks[20], (DEPTH, D_MODEL), 0.02),
        "w_up": nrm(ks[21], (DEPTH, D_MODEL, 2 * D_FF), D_MODEL ** -0.5),
        "ffn_conv_w": nrm(ks[22], (DEPTH, FFN_CONV, 2 * D_FF), FFN_CONV ** -0.5),
        "ffn_conv_b": nrm(ks[23], (DEPTH, 2 * D_FF), 0.02),
        "w_down": nrm(ks[24], (DEPTH, D_FF, D_MODEL), D_FF ** -0.5),
        "norm3_g": 1.0 + nrm(ks[25], (DEPTH, D_MODEL), 0.02),
        "w_pe": nrm(ks[26], (DEPTH, PLE_DIM, D_MODEL), PLE_DIM ** -0.5),
        "w_pg": nrm(ks[27], (DEPTH, D_MODEL, D_MODEL), D_MODEL ** -0.5),
        "final_g": 1.0 + nrm(ks[28], (D_MODEL,), 0.02),
    }


def reference(x, p, norm1_g, w_in, a_ln_g, a_ln_b, a_ws, a_bs, b_conv_w, b_conv_b, b_wa, b_ba, b_wx, b_bx, b_lam, c_lb, c_norm_g, d_w, d_scale, w_out, norm2_g, w_up, ffn_conv_w, ffn_conv_b, w_down, norm3_g, w_pe, w_pg, final_g):
    lbs = jnp.cumsum(jax.nn.softmax(c_lb.astype(jnp.float32), axis=0), axis=0)
    lbs = lbs - lbs[0:1]
    for l in range(DEPTH):
        h = rms_norm(x, norm1_g[l])
        z = h @ w_in[l]
        y_a = gmlp_mixer(z[..., :OFF_B], a_ln_g[l], a_ln_b[l], a_ws[l], a_bs[l])
        y_b = rglru_mixer(z[..., OFF_B:OFF_B + W_GRP], z[..., OFF_B + W_GRP:OFF_C],
                          b_conv_w[l], b_conv_b[l], b_wa[l], b_ba[l], b_wx[l], b_bx[l], b_lam[l])
        zc = z[..., OFF_C:OFF_D]
        y_c = hgrn2_mixer(zc[..., :W_GRP], zc[..., W_GRP:2 * W_GRP], zc[..., 2 * W_GRP:3 * W_GRP],
                          zc[..., 3 * W_GRP:], lbs[l], c_norm_g[l])
        y_d = pool_mixer(z[..., OFF_D:], d_w[l], d_scale[l])
        mix = jnp.concatenate([y_a, y_b, y_c, y_d], axis=-1)
        x = x + mix @ w_out[l]
        hf = rms_norm(x, norm2_g[l]) @ w_up[l]
        hf = causal_dwconv(hf, ffn_conv_w[l], ffn_conv_b[l])
        gt, val = jnp.split(hf, 2, axis=-1)
        x = x + (jax.nn.gelu(gt) * val) @ w_down[l]
        gate = jax.nn.sigmoid(rms_norm(x, norm3_g[l]) @ w_pg[l])
        x = x + (p[l] @ w_pe[l]) * gate
    return rms_norm(x, final_g)
```

```python
from contextlib import ExitStack
import numpy as np
import concourse.bass as bass
import concourse.mybir as mybir
from concourse.bass_utils import run_bass_kernel_spmd

F32 = mybir.dt.float32
BF16 = mybir.dt.bfloat16
AF = mybir.ActivationFunctionType
ALU = mybir.AluOpType

ENGS = ["pe", "act", "dve", "pool", "sp"]
D = 1024
DEPTH = 4
SEQ = 4096
BATCH = 4
DFF = 2816
EPS = 1e-6
POOLW = (2, 4, 8, 16)
GELU_C2 = 1.5957691216057308 * 0.044715
GELU_K = 1.0 / 0.044715


class Prog:
    def __init__(self, nc, stack, same_engine_sync=True):
        self.nc = nc
        self.stack = stack
        self.ops = {e: [] for e in ENGS}
        self.cnt = {}
        self.waited = {e: {} for e in ENGS}
        self.last_w = {}
        self.readers = {}
        self.semh = {}
        self.epoch = 0
        self.same_engine_sync = same_engine_sync
        self.last_tok = {}

    def sb(self, name, shape, dt=F32):
        return self.stack.enter_context(self.nc.sbuf_tensor("sb_" + name, list(shape), dt))

    def ps(self, name, shape, dt=F32):
        return self.stack.enter_context(self.nc.psum_tensor("pp_" + name, list(shape), dt))

    def _sem(self, name):
        if name not in self.semh:
            self.semh[name] = self.stack.enter_context(self.nc.semaphore(name.replace("@", "_")))
        return self.semh[name]

    def new_epoch(self):
        self.epoch += 1

    def op(self, eng, fn, reads=(), writes=(), dma_sem=None, wait_prev=False):
        deps = {}

        def add(tokd):
            for s, v in tokd.items():
                if deps.get(s, 0) < v:
                    deps[s] = v

        for k in reads:
            t = self.last_w.get(k)
            if t:
                add({t[0]: t[1]})
        for k in writes:
            t = self.last_w.get(k)
            if t:
                add({t[0]: t[1]})
            add(self.readers.get(k, {}))
        waits = []
        for s, v in deps.items():
            if (not self.same_engine_sync or eng == "pe") and s.split("@")[0] == eng:
                continue
            if self.waited[eng].get(s, 0) >= v:
                continue
            self.waited[eng][s] = v
            waits.append((s, v))
        if wait_prev and self.last_tok.get(eng):
            ps_, pv_ = self.last_tok[eng]
            if self.waited[eng].get(ps_, 0) < pv_:
                self.waited[eng][ps_] = pv_
                waits.append((ps_, pv_))
        if dma_sem is None:
            sname = f"{eng}@{self.epoch}"
            inc = 1
        else:
            sname = "dma_" + dma_sem
            inc = 16
        self.cnt[sname] = self.cnt.get(sname, 0) + inc
        tok = (sname, self.cnt[sname])
        self._sem(sname)
        if dma_sem is None:
            self.last_tok[eng] = tok
        self.ops[eng].append((waits, fn, sname, inc))
        for k in reads:
            r = self.readers.setdefault(k, {})
            if r.get(tok[0], 0) < tok[1]:
                r[tok[0]] = tok[1]
        for k in writes:
            self.last_w[k] = tok
            self.readers[k] = {}
        return tok

    def wait_tokens(self, eng, toks):
        self.ops[eng].append(([(s, v) for s, v in toks], None, None, 0))

    def emit(self):
        prog = self

        def run(e, eng):
            for waits, fn, sname, inc in prog.ops[e]:
                for s, v in waits:
                    eng.wait_ge(prog.semh[s], v)
                if fn is None:
                    continue
                fn(eng).then_inc(prog.semh[sname], inc)

        with self.nc.Block() as block:
            @block.tensor
            def _(eng):
                run("pe", eng)

            @block.scalar
            def _(eng):
                run("act", eng)

            @block.vector
            def _(eng):
                run("dve", eng)

            @block.gpsimd
            def _(eng):
                run("pool", eng)

            @block.sync
            def _(eng):
                run("sp", eng)


class Cols:
    def __init__(self):
        self.off = {}
        self.n = 0
        self.parts = []

    def add(self, name, arr):
        arr = np.asarray(arr, np.float32)
        if arr.ndim == 1:
            arr = arr[:, None]
        assert arr.shape[0] == 128
        self.off[name] = (self.n, arr.shape[1])
        self.n += arr.shape[1]
        self.parts.append(arr)

    def build(self):
        return np.ascontiguousarray(np.concatenate(self.parts, axis=1))


def vec_cols(v):
    v = np.asarray(v, np.float32)
    return np.ascontiguousarray(v.reshape(-1, 128).T)


def cst_layout(inp=None, L=DEPTH):
    C = Cols()
    z = lambda *s: np.zeros(s, np.float32)
    g = (lambda n, l: inp[n][l]) if inp is not None else None
    C.add("one", np.ones((128, 1), np.float32))
    C.add("eps", np.full((128, 1), EPS, np.float32))
    for l in range(L):
        for nm, key in (("g1", "norm1_g"), ("g2", "norm2_g"), ("g3", "norm3_g")):
            C.add(f"{nm}_{l}", vec_cols(g(key, l)) if inp is not None else z(128, 8))
        C.add(f"lng_{l}", vec_cols(g("a_ln_g", l)) if inp is not None else z(128, 2))
        C.add(f"lnb_{l}", vec_cols(g("a_ln_b", l)) if inp is not None else z(128, 2))
        if inp is not None:
            cw = g("b_conv_w", l)
            bcw = np.concatenate([vec_cols(cw[k])[:, j:j + 1] for j in range(2) for k in range(4)], axis=1)
        else:
            bcw = z(128, 8)
        C.add(f"bcw_{l}", bcw)
        for nm, key in (("bcb", "b_conv_b"), ("bba", "b_ba"), ("bbx", "b_bx"), ("lam", "b_lam"), ("dsc", "d_scale")):
            C.add(f"{nm}_{l}", vec_cols(g(key, l)) if inp is not None else z(128, 2))
        C.add(f"cng_{l}", np.tile(g("c_norm_g", l), 2)[:, None] if inp is not None else z(128, 1))
        if inp is not None:
            fw = g("ffn_conv_w", l)
            fcw = np.stack([vec_cols(fw[k]) for k in range(3)], axis=2).reshape(128, 132)
        else:
            fcw = z(128, 132)
        C.add(f"fcw_{l}", fcw)
        C.add(f"fcb_{l}", vec_cols(g("ffn_conv_b", l)) if inp is not None else z(128, 44))
    if inp is not None:
        clb = np.stack([vec_cols(inp["c_lb"][l]) for l in range(L)], axis=2).reshape(128, 2 * L)
    else:
        clb = z(128, 2 * L)
    C.add("clb", clb)
    C.add("gf", vec_cols(inp["final_g"]) if inp is not None else z(128, 8))
    rw = np.zeros((128, 2), np.float32)
    corr = np.zeros((128, 32), np.float32)
    for j in range(2):
        for hh in range(2):
            w = POOLW[2 * j + hh]
            rw[hh * 64:(hh + 1) * 64, j] = 1.0 / w
            for t in range(16):
                corr[hh * 64:(hh + 1) * 64, j * 16 + t] = w / min(t + 1, w)
    C.add("rw", rw)
    C.add("corr", corr)
    return C


def const_mats():
    km = np.zeros((4, 128, 128), np.float32)
    s = np.arange(128)[:, None]
    t = np.arange(128)[None, :]
    km[0] = (t >= s)
    km[1] = (s // 64 == t // 64)
    km[2] = np.eye(128)
    km[3, :, 0:64] = (np.arange(64)[None, :] >= (np.arange(128) % 64)[:, None])
    km[3, :, 64:128] = 1.0
    km[3, :, 64] = 0.0
    return km


def blockdiag(a, b):
    m = np.zeros((128, 128), np.float32)
    m[0:64, 0:64] = a
    m[64:128, 64:128] = b
    return m


def prep_core_inputs(inp, b, S, L=DEPTH):
    f = lambda a: np.ascontiguousarray(a, dtype=np.float32)
    xT = f(inp["x"][b, :S].T.reshape(8, 128, S))
    pT = f(np.stack([inp["p"][l, b, :S].T.reshape(2, 128, S) for l in range(L)]))
    wA = np.zeros((L, 40, 128, 2048), np.float32)
    wB = np.zeros((L, 16, 128, 1408), np.float32)
    mats = np.zeros((L, 12, 128, 128), np.float32)
    for l in range(L):
        def kview(w):
            K, N = w.shape
            return w.reshape(K // 128, 128, N).transpose(1, 0, 2)
        win = kview(inp["w_in"][l])
        for gidx in range(9):
            wA[l, gidx] = win[:, :, gidx * 256:(gidx + 1) * 256].reshape(128, 2048)
        wo = kview(inp["w_out"][l])
        for m in range(4):
            wA[l, 9 + m] = wo[:, :, m * 256:(m + 1) * 256].reshape(128, 2048)
        wu = kview(inp["w_up"][l])
        for j in range(22):
            wA[l, 13 + j] = np.concatenate([wu[:, :, j * 128:(j + 1) * 128], wu[:, :, DFF + j * 128:DFF + (j + 1) * 128]], axis=2).reshape(128, 2048)
        wg = kview(inp["w_pg"][l])
        for m in range(4):
            wA[l, 35 + m] = wg[:, :, m * 256:(m + 1) * 256].reshape(128, 2048)
        wA[l, 39] = kview(inp["w_pe"][l]).reshape(128, 2048)
        wd = kview(inp["w_down"][l])
        for m in range(8):
            for hf_ in range(2):
                wB[l, 2 * m + hf_] = wd[:, hf_ * 11:(hf_ + 1) * 11, m * 128:(m + 1) * 128].reshape(128, 1408)
        for h in range(4):
            mats[l, h] = inp["a_ws"][l, h].T
        for j in range(2):
            mats[l, 4 + j] = np.repeat(inp["a_bs"][l, 2 * j:2 * j + 2], 64, axis=0)
            mats[l, 6 + j] = blockdiag(inp["b_wa"][l, 2 * j], inp["b_wa"][l, 2 * j + 1])
            mats[l, 8 + j] = blockdiag(inp["b_wx"][l, 2 * j], inp["b_wx"][l, 2 * j + 1])
            mats[l, 10 + j] = blockdiag(inp["d_w"][l, 2 * j], inp["d_w"][l, 2 * j + 1])
    cst = cst_layout(inp, L).build()
    return {"xT": xT, "pT": pT, "wA": wA, "wB": wB, "mats": mats, "cst": cst, "kmat": const_mats()}


def build_program(S=SEQ, TT=512, L=DEPTH, R=8, dbg=False, same_engine_sync=True, stop=None, nodma=False, pool_conv=True, mixsel=None):
    assert TT == 512 and S % TT == 0
    NT = S // TT
    NB = TT // 128
    NH = TT // 512
    NCK = TT // 64
    CL = cst_layout(None, L)
    NCST = CL.n
    nc = bass.Bass("TRN2", target_bir_lowering=False)
    dram = lambda n, s, k="ExternalInput": nc.dram_tensor(n, list(s), F32, kind=k).ap()
    xT = dram("xT", [8, 128, S])
    pT = dram("pT", [L, 2, 128, S])
    wA = dram("wA", [L, 40, 128, 2048])
    wB = dram("wB", [L, 16, 128, 1408])
    matsD = dram("mats", [L, 12, 128, 128])
    cstD = dram("cst", [128, NCST])
    kmatD = dram("kmat", [4, 128, 128])
    outT = dram("outT", [8, 128, S], "ExternalOutput")
    dbgT = dram("dbgT", [8, 128, TT], "ExternalOutput") if dbg else None

    with ExitStack() as st:
        P = Prog(nc, st, same_engine_sync=same_engine_sync)
        sb, ps = P.sb, P.ps
        cst = sb("cst", [128, NCST])
        def cc(name, j=0, n=1):
            o, w = CL.off[name]
            return cst[:, o + j:o + j + n]
        kmat = sb("kmat", [128, 4, 128])
        mats = sb("matsf", [128, 12, 128])
        ones_bf = sb("ones_bf", [128, 128], BF16)
        bones_bf = sb("bones_bf", [128, 128], BF16)
        ident_bf = sb("ident_bf", [128, 128], BF16)
        rmask = sb("rmask", [128, TT])
        wm_bf = sb("wm_bf", [128, L * 4, 128], BF16)
        biasA = sb("biasA", [128, L * 2, 128])
        bd_bf = sb("bd_bf", [128, L * 6, 128], BF16)
        crg = sb("crg", [128, 2 * L])
        lbt = sb("lbt", [128, 2 * L])
        omlt = sb("omlt", [128, 2 * L])
        sm_t = [sb(f"smt{i}", [128, 2 * L]) for i in range(4)]
        st_xb = sb("st_xb", [128, L * 2, 3])
        st_h = sb("st_h", [128, L * 2])
        st_S = sb("st_S", [128, L, 128])
        st_xd = sb("st_xd", [128, L * 2, 15])
        st_hf = sb("st_hf", [128, L * 44, 2])
        x = [sb(f"x{c}", [128, TT]) for c in range(8)]
        h = [sb(f"h{c}", [128, TT], BF16) for c in range(8)]
        mix = [sb(f"mix{c}", [128, TT], BF16) for c in range(8)]
        actb = [sb(f"actb{c}", [128, TT], BF16) for c in range(22)]
        ring = [sb(f"ring{s}", [128, 2048], BF16) for s in range(R)]
        pbf = [[sb(f"pbf{a}_{k}", [128, TT], BF16) for k in range(2)] for a in range(2)]
        T = [sb(f"T{i}", [128, TT + 16]) for i in range(23)]
        sqb2 = [sb(f"sqb{i}", [128, TT], BF16) for i in range(2)]
        rstd = sb("rstd", [128, TT])
        vtok = [sb(f"vtok{i}", [128, 256]) for i in range(2)]
        vhat = [sb(f"vhat{i}", [128, 256], BF16) for i in range(NB)]
        bnst = sb("bnst", [128, NB, 6])
        bnmv = sb("bnmv", [128, NB, 2])
        bnr = sb("bnr", [128, NB])
        xcb2 = [sb(f"xcb{i}", [128, TT], BF16) for i in range(2)]
        Ktb = [sb(f"Ktb{j}", [128, TT], BF16) for j in range(2)]
        Qtb = [sb(f"Qtb{j}", [128, TT], BF16) for j in range(2)]
        Qbb = [sb(f"Qbb{j}", [128, TT], BF16) for j in range(2)]
        attb = [sb(f"attb{j}", [128, TT], BF16) for j in range(2)]
        Ktok = [sb(f"Ktok{i}", [128, 256], BF16) for i in range(NB)]
        Vtok = [sb(f"Vtok{i}", [128, 256], BF16) for i in range(NB)]
        Shist = sb("Shist", [128, NCK + 1, 128], BF16)
        dec = [sb(f"dec{j}", [128, NCK]) for j in range(2)]
        pooled2 = [sb(f"pooled{i}", [128, TT], BF16) for i in range(2)]
        NBK = 5
        psb = [ps(f"psb{i}", [128, 512]) for i in range(NBK)]
        ptok2 = [ps(f"ptok{i}", [128, 512]) for i in range(2)]
        ptr = ps("ptr", [128, 1024], BF16)
        bank_ctr = [0]

        reserved = set()

        def nb(reserve=False):
            while True:
                k = bank_ctr[0] % NBK
                bank_ctr[0] += 1
                if k not in reserved:
                    break
            if reserve:
                reserved.add(k)
            return k

        last_rg = [None]

        def mm(out, lhsT, rhs, start, stop, reads, writes, rg=None):
            force = rg is not None and last_rg[0] is not None and rg != last_rg[0]
            last_rg[0] = rg
            P.op("pe", lambda e: e.matmul(out, lhsT=lhsT, rhs=rhs, start=start, stop=stop), reads=reads, writes=writes, wait_prev=force)

        def act(out, in_, func, reads, writes, bias=None, scale=1.0):
            if bias is None:
                P.op("act", lambda e: e.activation(out=out, in_=in_, func=func, scale=scale), reads=reads, writes=writes)
            else:
                P.op("act", lambda e: e.activation(out=out, in_=in_, func=func, bias=bias, scale=scale), reads=reads, writes=writes)

        def tt(out, a, b, op, reads, writes, eng="dve"):
            P.op(eng, lambda e: e.tensor_tensor(out=out, in0=a, in1=b, op=op), reads=reads, writes=writes)

        def ts(out, a, s1, s2, op0, op1, reads, writes):
            if op1 is None:
                P.op("dve", lambda e: e.tensor_scalar(out=out, in0=a, scalar1=s1, scalar2=None, op0=op0), reads=reads, writes=writes)
            else:
                P.op("dve", lambda e: e.tensor_scalar(out=out, in0=a, scalar1=s1, scalar2=s2, op0=op0, op1=op1), reads=reads, writes=writes)

        def stt(out, a, s, b, op0, op1, reads, writes):
            P.op("dve", lambda e: e.scalar_tensor_tensor(out=out, in0=a, scalar=s, in1=b, op0=op0, op1=op1), reads=reads, writes=writes)

        def cp(out, in_, reads, writes, eng="dve"):
            P.op(eng, lambda e: e.tensor_copy(out=out, in_=in_), reads=reads, writes=writes)

        def gelu_tanh(out, kout, src, ksrc, tmp, ktmp):
            act(tmp, src, AF.Square, ksrc, [ktmp])
            stt(tmp, tmp, GELU_K, src, ALU.add, ALU.mult, [ktmp] + ksrc, [ktmp])
            act(tmp, tmp, AF.Sigmoid, [ktmp], [ktmp], scale=GELU_C2)
            tt(out, tmp, src, ALU.mult, [ktmp] + ksrc, [kout])

        def recip(out, in_, reads, writes):
            P.op("dve", lambda e: e.reciprocal(out=out, in_=in_), reads=reads, writes=writes)

        ring_ctr = [0]

        def load_piece(l, idx):
            s = ring_ctr[0] % R
            ring_ctr[0] += 1
            if nodma:
                return s
            if idx < 40:
                src, n = wA[l, idx], 2048
            else:
                src, n = wB[l, idx - 40], 1408
            P.op("pool", lambda e: e.dma_start(out=ring[s][:, 0:n], in_=src), writes=[f"ring{s}"], dma_sem=f"ring{s}")
            return s

        def rview(s, k, n):
            return ring[s][:, 0:k * n].rearrange("p (k n) -> p k n", k=k)

        if nodma:
            for s_ in range(R):
                P.op("dve", lambda e, s_=s_: e.memset(ring[s_][:], 0.0), writes=[f"ring{s_}"])
        P.op("sp", lambda e: e.dma_start(out=cst[:], in_=cstD), writes=["cst"], dma_sem="cst")
        P.op("sp", lambda e: e.dma_start(out=kmat[:], in_=kmatD.rearrange("k p n -> p k n")), writes=["kmat"], dma_sem="kmat")
        P.op("dve", lambda e: e.memset(ones_bf[:], 1.0), writes=["ones"])
        cp(bones_bf[:], kmat[:, 1, :], ["kmat"], ["bones"])
        cp(ident_bf[:], kmat[:, 2, :], ["kmat"], ["ident"])
        cp(rmask[:].rearrange("p (c t) -> p c t", t=64), kmat[:, 3, 64:128].unsqueeze(1).broadcast_to([128, NCK, 64]), ["kmat"], ["rmask"])
        for nm, tl in ((["st_xb"], st_xb), (["st_h"], st_h), (["st_S0", "st_S1"], st_S), (["st_xd"], st_xd), ([f"hf{l_}_{c_}" for l_ in range(L) for c_ in range(44)], st_hf)):
            P.op("pool", lambda e, tl=tl: e.memset(tl[:], 0.0), writes=nm)
        for l in range(L):
            P.op("sp", lambda e, l=l: e.dma_start(out=mats[:], in_=matsD[l].rearrange("k p n -> p k n")), writes=["mats"], dma_sem="mats")
            for hd in range(4):
                tt(wm_bf[:, l * 4 + hd, :], mats[:, hd, :], kmat[:, 0, :], ALU.mult, ["mats", "kmat"], ["wm"])
            for q in range(6):
                cp(bd_bf[:, l * 6 + q, :], mats[:, 6 + q, :], ["mats"], ["bd"])
            for j in range(2):
                bk = nb()
                for hh in range(2):
                    mm(psb[bk][:, hh * 128:(hh + 1) * 128], ones_bf[:], wm_bf[:, l * 4 + 2 * j + hh, :], True, True, ["ones", "wm"], [f"ps{bk}"])
                for hh in range(2):
                    sl = slice(hh * 64, (hh + 1) * 64)
                    stt(biasA[sl, l * 2 + j, :], psb[bk][sl, hh * 128:(hh + 1) * 128], cc(f"lnb_{l}", j)[sl], mats[sl, 4 + j, :],
                        ALU.mult, ALU.add, [f"ps{bk}", "cst", "mats"], ["biasA"])
        lam_all = sm_t[0]
        for l in range(L):
            cp(lam_all[:, 2 * l:2 * l + 2], cc(f"lam_{l}", 0, 2), ["cst"], ["smt0"])
        ts(sm_t[1][:], lam_all[:], -1.0, 0.0, ALU.mult, ALU.max, ["smt0"], ["smt1"])
        ts(sm_t[2][:], lam_all[:], -1.0, None, ALU.mult, None, ["smt0"], ["smt2"])
        tt(sm_t[2][:], sm_t[2][:], lam_all[:], ALU.min, ["smt2", "smt0"], ["smt2"])
        act(sm_t[2][:], sm_t[2][:], AF.Exp, ["smt2"], ["smt2"])
        act(sm_t[2][:], sm_t[2][:], AF.Ln, ["smt2", "cst"], ["smt2"], bias=cc("one"))
        tt(sm_t[1][:], sm_t[1][:], sm_t[2][:], ALU.add, ["smt1", "smt2"], ["smt1"])
        ts(crg[:], sm_t[1][:], -8.0, None, ALU.mult, None, ["smt1"], ["crg"])
        clb = cc("clb", 0, 2 * L).rearrange("p (j l) -> p j l", l=L)
        e3 = sm_t[3][:].rearrange("p (j l) -> p j l", l=L)
        mx = sm_t[0][:, 0:2]
        P.op("dve", lambda e: e.tensor_reduce(out=mx, in_=clb, op=ALU.max, axis=mybir.AxisListType.X), reads=["cst"], writes=["smt0"])
        tt(e3, clb, mx.unsqueeze(2).broadcast_to([128, 2, L]), ALU.subtract, ["cst", "smt0"], ["smt3"])
        act(sm_t[3][:], sm_t[3][:], AF.Exp, ["smt3"], ["smt3"])
        sm = sm_t[1][:, 0:2]
        P.op("dve", lambda e: e.tensor_reduce(out=sm, in_=e3, op=ALU.add, axis=mybir.AxisListType.X), reads=["smt3"], writes=["smt1"])
        recip(sm, sm, ["smt1"], ["smt1"])
        tt(e3, e3, sm.unsqueeze(2).broadcast_to([128, 2, L]), ALU.mult, ["smt3", "smt1"], ["smt3"])
        lb3 = lbt[:].rearrange("p (j l) -> p j l", l=L)
        P.op("dve", lambda e: e.memset(lbt[:], 0.0), writes=["lbt"])
        for l in range(1, L):
            tt(lb3[:, :, l], lb3[:, :, l - 1], e3[:, :, l], ALU.add, ["lbt", "smt3"], ["lbt"])
        ts(omlt[:], lbt[:], -1.0, 1.0, ALU.mult, ALU.add, ["lbt"], ["omlt"])

        def rmsnorm(gname):
            for hf in range(NH):
                cs = slice(hf * 512, (hf + 1) * 512)
                bk = nb()
                for c in range(8):
                    act(sqb2[c % 2][:, cs], x[c][:, cs], AF.Square, [f"x{c}"], [f"sqb{c % 2}"])
                    mm(psb[bk][:], ones_bf[:], sqb2[c % 2][:, cs], c == 0, c == 7, ["ones", f"sqb{c % 2}"], [f"ps{bk}"])
                act(rstd[:, cs], psb[bk][:], AF.Sqrt, [f"ps{bk}", "cst"], ["rstd"], bias=cc("eps"), scale=1.0 / D)
                recip(rstd[:, cs], rstd[:, cs], ["rstd"], ["rstd"])
            for c in range(8):
                stt(h[c][:], x[c][:], cc(gname, c), rstd[:], ALU.mult, ALU.mult, [f"x{c}", "cst", "rstd"], [f"h{c}"])

        def proj_fm(slot, col0, bk, hf):
            wv = rview(slot, 8, 256)
            for kc in range(8):
                mm(psb[bk][:], wv[:, kc, col0:col0 + 128], h[kc][:, hf * 512:(hf + 1) * 512], kc == 0, kc == 7,
                   [f"ring{slot}", f"h{kc}"], [f"ps{bk}"])

        def proj_tm(slot, blk, half):
            wv = rview(slot, 8, 256)
            for kc in range(8):
                mm(ptok2[half][:, 0:256], h[kc][:, blk * 128:(blk + 1) * 128], wv[:, kc, :], kc == 0, kc == 7,
                   [f"ring{slot}", f"h{kc}"], [f"ptok{half}"])

        tok_ctr = [0]

        def rr(*gens):
            gens = list(gens)
            while gens:
                for g in list(gens):
                    try:
                        next(g)
                        yield
                    except StopIteration:
                        gens.remove(g)

        def interleave(*gens):
            gens = list(gens)
            while gens:
                for g in list(gens):
                    try:
                        next(g)
                    except StopIteration:
                        gens.remove(g)

        def step(l, i):
            t0 = i * TT
            first = (i == 0)
            if l == 0:
                for c in range(8):
                    P.op("sp", lambda e, c=c: e.dma_start(out=x[c][:], in_=xT[c, :, t0:t0 + TT]), writes=[f"x{c}"], dma_sem=f"x{c}")
            pa = (l + i * L) % 2
            for k in range(2):
                P.op("pool", lambda e, k=k: e.dma_start(out=pbf[pa][k][:], in_=pT[l, k, :, t0:t0 + TT]), writes=[f"p{pa}_{k}"], dma_sem=f"p{pa}_{k}")
            if stop == 'pro':
                return []
            rmsnorm(f"g1_{l}")
            if stop == 'norm':
                return []

            s_u = load_piece(l, 0)
            u = [T[0], T[1]]
            for j in range(2):
                for hf in range(NH):
                    bk = nb()
                    proj_fm(s_u, j * 128, bk, hf)
                    gelu_tanh(u[j][:, hf * 512:(hf + 1) * 512], f"T{j}", psb[bk][:], [f"ps{bk}"], T[2][:, 0:512], "T2")
            s_v = load_piece(l, 1)
            for blk in range(NB):
                half = tok_ctr[0] % 2
                tok_ctr[0] += 1
                proj_tm(s_v, blk, half)
                vt = vtok[half]
                gelu_tanh(vt[:], f"vtok{half}", ptok2[half][:, 0:256], [f"ptok{half}"], T[2][:, 256 * half:256 * (half + 1)], "T2")
                P.op("dve", lambda e, vt=vt, blk=blk: e.bn_stats(out=bnst[:, blk, :], in_=vt[:]), reads=[f"vtok{half}"], writes=["bnst"])
                P.op("dve", lambda e, blk=blk: e.bn_aggr(out=bnmv[:, blk, :], in_=bnst[:, blk, :]), reads=["bnst"], writes=["bnmv"])
                act(bnr[:, blk:blk + 1], bnmv[:, blk, 1:2], AF.Sqrt, ["bnmv", "cst"], ["bnr"], bias=cc("eps"))
                recip(bnr[:, blk:blk + 1], bnr[:, blk:blk + 1], ["bnr"], ["bnr"])
                ts(vhat[blk][:], vt[:], bnmv[:, blk, 0:1], bnr[:, blk:blk + 1], ALU.subtract, ALU.mult,
                   [f"vtok{half}", "bnmv", "bnr"], [f"vhat{blk}"])
            for j in range(2):
                for hf in range(NH):
                    bk = nb()
                    for bb in range(4):
                        blk = hf * 4 + bb
                        for hh in range(2):
                            hd = 2 * j + hh
                            mm(psb[bk][hh * 64:(hh + 1) * 64, bb * 128:(bb + 1) * 128], vhat[blk][:, hd * 64:(hd + 1) * 64],
                               wm_bf[:, l * 4 + hd, :], True, True, [f"vhat{blk}", "wm"], [f"ps{bk}"])
                    ya = T[2]
                    cs = slice(hf * 512, (hf + 1) * 512)
                    stt(ya[:, 0:512].rearrange("p (b t) -> p b t", t=128), psb[bk][:].rearrange("p (b t) -> p b t", t=128), cc(f"lng_{l}", j),
                        biasA[:, l * 2 + j, :].unsqueeze(1).broadcast_to([128, 4, 128]), ALU.mult, ALU.add,
                        [f"ps{bk}", "cst", "biasA"], ["T2"])
                    tt(mix[j][:, cs], ya[:, 0:512], u[j][:, cs], ALU.mult, ["T2", f"T{j}"], [f"mix{j}"])

            if stop == 'A':
                return []
            s_xb = load_piece(l, 2)
            s_gb = load_piece(l, 3)

            def genB(j, ti):
                xbuf, xc, r_, ig, hs, gg, tmp = [T[k] for k in ti]
                kxb, kxc, kr, kig, khs, kgg, ktm = [f"T{k}" for k in ti]
                xcb_, kxcb = xcb2[j], f"xcb{j}"
                cp(xbuf[:, 0:3], st_xb[:, l * 2 + j, :], ["st_xb"], [kxb])
                for hf in range(NH):
                    bk = nb()
                    proj_fm(s_xb, j * 128, bk, hf)
                    act(xbuf[:, 3 + hf * 512:3 + (hf + 1) * 512], psb[bk][:], AF.Identity, [f"ps{bk}"], [kxb])
                yield
                cp(st_xb[:, l * 2 + j, :], xbuf[:, TT:TT + 3], [kxb], ["st_xb"])
                cw = lambda k: cc(f"bcw_{l}", j * 4 + k)
                act(xc[:, 0:TT], xbuf[:, 3:3 + TT], AF.Identity, [kxb, "cst"], [kxc], bias=cc(f"bcb_{l}", j), scale=cw(3))
                yield
                for k in (2, 1, 0):
                    stt(xc[:, 0:TT], xbuf[:, k:k + TT], cw(k), xc[:, 0:TT], ALU.mult, ALU.add, [kxb, "cst", kxc], [kxc])
                    yield
                act(xcb_[:], xc[:, 0:TT], AF.Identity, [kxc], [kxcb])
                yield
                for hf in range(NH):
                    cs = slice(hf * 512, (hf + 1) * 512)
                    bk = nb()
                    mm(psb[bk][:], bd_bf[:, l * 6 + j, :], xcb_[:, cs], True, True, ["bd", kxcb], [f"ps{bk}"])
                    act(r_[:, cs], psb[bk][:], AF.Sigmoid, [f"ps{bk}", "cst"], [kr], bias=cc(f"bba_{l}", j))
                    bk = nb()
                    mm(psb[bk][:], bd_bf[:, l * 6 + 2 + j, :], xcb_[:, cs], True, True, ["bd", kxcb], [f"ps{bk}"])
                    act(ig[:, cs], psb[bk][:], AF.Sigmoid, [f"ps{bk}", "cst"], [kig], bias=cc(f"bbx_{l}", j))
                yield
                act(r_[:, 0:TT], r_[:, 0:TT], AF.Exp, [kr, "crg"], [kr], scale=crg[:, 2 * l + j:2 * l + j + 1])
                tt(ig[:, 0:TT], ig[:, 0:TT], xc[:, 0:TT], ALU.mult, [kig, kxc], [kig])
                yield
                tt(tmp[:, 0:TT], r_[:, 0:TT], r_[:, 0:TT], ALU.mult, [kr], [ktm])
                yield
                act(tmp[:, 0:TT], tmp[:, 0:TT], AF.Sqrt, [ktm, "cst"], [ktm], bias=cc("one"), scale=-1.0)
                for hf in range(NH):
                    cs = slice(hf * 512, (hf + 1) * 512)
                    bk = nb()
                    proj_fm(s_gb, j * 128, bk, hf)
                    gelu_tanh(gg[:, cs], kgg, psb[bk][:], [f"ps{bk}"], hs[:, cs], khs)
                yield
                tt(ig[:, 0:TT], ig[:, 0:TT], tmp[:, 0:TT], ALU.mult, [kig, ktm], [kig])
                yield
                stt(ig[:, 0:1], r_[:, 0:1], st_h[:, l * 2 + j:l * 2 + j + 1], ig[:, 0:1], ALU.mult, ALU.add, [kr, "st_h", kig], [kig])
                P.op("dve", lambda e: e.tensor_tensor_scan(out=hs[:, 0:TT], data0=r_[:, 0:TT], data1=ig[:, 0:TT],
                                                           initial=0.0, op0=ALU.mult, op1=ALU.add),
                     reads=[kr, kig], writes=[khs])
                yield
                cp(st_h[:, l * 2 + j:l * 2 + j + 1], hs[:, TT - 1:TT], [khs], ["st_h"])
                tt(mix[2 + j][:], hs[:, 0:TT], gg[:, 0:TT], ALU.mult, [khs, kgg], [f"mix{2 + j}"])


            if stop == 'B':
                return []
            s_q = load_piece(l, 4)
            s_f = load_piece(l, 5)
            s_i = load_piece(l, 6)
            s_g = load_piece(l, 7)
            for blk in range(NB):
                half = tok_ctr[0] % 2
                tok_ctr[0] += 1
                proj_tm(s_i, blk, half)
                act(Vtok[blk][:], ptok2[half][:, 0:256], AF.Identity, [f"ptok{half}"], [f"Vtok{blk}"])

            def genC1(j, ti):
                qs, fg, lf, kf, bb_, rb, ex, gsj = [T[k] for k in ti]
                kqs, kfg, klf, kkf, kbb, krb, kex, kgs = [f"T{k}" for k in ti]
                for hf in range(NH):
                    cs = slice(hf * 512, (hf + 1) * 512)
                    bk = nb()
                    proj_fm(s_f, j * 128, bk, hf)
                    act(fg[:, cs], psb[bk][:], AF.Sigmoid, [f"ps{bk}"], [kfg])
                    bk = nb()
                    proj_fm(s_q, j * 128, bk, hf)
                    act(qs[:, cs], psb[bk][:], AF.Sigmoid, [f"ps{bk}"], [kqs])
                    tt(qs[:, cs], qs[:, cs], psb[bk][:], ALU.mult, [kqs, f"ps{bk}"], [kqs])
                yield
                jl = j * L + l
                ts(fg[:, 0:TT], fg[:, 0:TT], omlt[:, jl:jl + 1], lbt[:, jl:jl + 1], ALU.mult, ALU.add, [kfg, "omlt", "lbt"], [kfg])
                yield
                act(lf[:, 0:TT], fg[:, 0:TT], AF.Ln, [kfg], [klf])
                ts(kf[:, 0:TT], fg[:, 0:TT], -1.0, 1.0, ALU.mult, ALU.add, [kfg], [kkf])
                yield
                P.op("dve", lambda e: e.tensor_tensor_scan(out=bb_[:, 0:TT], data0=rmask[:], data1=lf[:, 0:TT], initial=0.0,
                                                           op0=ALU.mult, op1=ALU.add), reads=["rmask", klf], writes=[kbb])
                for hf in range(NH):
                    cs = slice(hf * 512, (hf + 1) * 512)
                    bk = nb()
                    proj_fm(s_g, j * 128, bk, hf)
                    act(gsj[:, cs], psb[bk][:], AF.Sigmoid, [f"ps{bk}"], [kgs])
                    tt(gsj[:, cs], gsj[:, cs], psb[bk][:], ALU.mult, [kgs, f"ps{bk}"], [kgs])
                yield
                b3 = bb_[:, 0:TT].rearrange("p (c t) -> p c t", t=64)
                tt(rb[:, 0:TT].rearrange("p (c t) -> p c t", t=64), b3[:, :, 63:64].broadcast_to([128, NCK, 64]), b3, ALU.subtract,
                   [kbb], [krb])
                act(dec[j][:].unsqueeze(2), b3[:, :, 63:64], AF.Exp, [kbb], [f"dec{j}"])
                yield
                act(ex[:, 0:TT], rb[:, 0:TT], AF.Exp, [krb], [kex])
                yield
                tt(Ktb[j][:], kf[:, 0:TT], ex[:, 0:TT], ALU.mult, [kkf, kex], [f"Ktb{j}"])
                yield
                act(ex[:, 0:TT], rb[:, 0:TT], AF.Exp, [krb], [kex], scale=-1.0)
                for blk in range(NB):
                    po = ((blk % 4) * 2 + j) * 128
                    P.op("pe", lambda e, blk=blk, po=po: e.transpose(ptr[:, po:po + 128], Ktb[j][:, blk * 128:(blk + 1) * 128], ident_bf[:]),
                         reads=[f"Ktb{j}", "ident"], writes=["ptr"])
                    cp(Ktok[blk][:, j * 128:(j + 1) * 128], ptr[:, po:po + 128], ["ptr"], [f"Ktok{blk}"])
                yield
                tt(Qtb[j][:], qs[:, 0:TT], ex[:, 0:TT], ALU.mult, [kqs, kex], [f"Qtb{j}"])
                yield
                act(ex[:, 0:TT], bb_[:, 0:TT], AF.Exp, [kbb], [kex])
                yield
                tt(Qbb[j][:], qs[:, 0:TT], ex[:, 0:TT], ALU.mult, [kqs, kex], [f"Qbb{j}"])

            tiC = [list(range(0, 8)), list(range(8, 16))]
            gs = [T[tiC[0][7]], T[tiC[1][7]]]
            kgs_ = [f"T{tiC[0][7]}", f"T{tiC[1][7]}"]

            def genCmid():
                yield
                nub = (NCK * 128) // 512
                ub = [nb(True) for _ in range(nub)]
                for c in [c_ for par_ in range(2) for c_ in range(NCK) if c_ % 2 == par_]:
                    blk, cpar = c // 2, c % 2
                    bk = ub[(c * 128) // 512]
                    co = (c * 128) % 512
                    for j in range(2):
                        for hh in range(2):
                            hd = 2 * j + hh
                            mm(psb[bk][hh * 64:(hh + 1) * 64, co + j * 64:co + (j + 1) * 64],
                               Ktok[blk][cpar * 64:(cpar + 1) * 64, hd * 64:(hd + 1) * 64],
                               Vtok[blk][cpar * 64:(cpar + 1) * 64, hd * 64:(hd + 1) * 64], True, True,
                               [f"Ktok{blk}", f"Vtok{blk}"], [f"ps{bk}"], rg=cpar)
                for j in range(2):
                    for hf in range(NH):
                        bk = nb()
                        for hh in range(2):
                            for cl in range(8):
                                c = hf * 8 + cl
                                cpar, ccn = c % 2, cl // 2
                                sl = slice(hh * 64, (hh + 1) * 64)
                                mm(psb[bk][cpar * 64:(cpar + 1) * 64, (hh * 4 + ccn) * 64:(hh * 4 + ccn + 1) * 64],
                                   Ktb[j][sl, c * 64:(c + 1) * 64], Qtb[j][sl, c * 64:(c + 1) * 64], True, True,
                                   [f"Ktb{j}", f"Qtb{j}"], [f"ps{bk}"], rg=hh)
                        tt(attb[j][:, hf * 512:(hf + 1) * 512].rearrange("p (a t) -> p a t", t=64), psb[bk][:].rearrange("p (a t) -> p a t", t=64),
                           kmat[:, 3, 0:64].unsqueeze(1).broadcast_to([128, 8, 64]), ALU.mult, [f"ps{bk}", "kmat"], [f"attb{j}"])
                cp(Shist[:, 0, :], st_S[:, l, :], ["st_S0", "st_S1"], ["Shist"])
                for c in range(NCK):
                    bk = ub[(c * 128) // 512]
                    co = (c * 128) % 512
                    for j in range(2):
                        stt(st_S[:, l, j * 64:(j + 1) * 64], st_S[:, l, j * 64:(j + 1) * 64], dec[j][:, c:c + 1],
                            psb[bk][:, co + j * 64:co + (j + 1) * 64], ALU.mult, ALU.add, [f"st_S{j}", f"dec{j}", f"ps{bk}"], [f"st_S{j}"])
                    act(Shist[:, c + 1, :], st_S[:, l, :], AF.Identity, ["st_S0", "st_S1"], ["Shist"])
                    yield
                for k_ in ub:
                    reserved.discard(k_)

            def genC3(j, ti):
                orst, t1 = T[ti[0]], T[ti[1]]
                korst, kt1 = f"T{ti[0]}", f"T{ti[1]}"
                osq, kosq = sqb2[j], f"sqb{j}"
                for hf in range(NH):
                    cs = slice(hf * 512, (hf + 1) * 512)
                    bk = nb(True)
                    for cl in range(8):
                        c = hf * 8 + cl
                        blk, cpar, ccn = c // 2, c % 2, cl // 2
                        for hh in range(2):
                            hd = 2 * j + hh
                            sl = slice(hh * 64, (hh + 1) * 64)
                            outp = psb[bk][sl, cl * 64:(cl + 1) * 64]
                            mm(outp, Shist[sl, c, j * 64:(j + 1) * 64], Qbb[j][sl, c * 64:(c + 1) * 64], True, False,
                               ["Shist", f"Qbb{j}"], [f"ps{bk}"], rg=hh)
                            mm(outp, Vtok[blk][cpar * 64:(cpar + 1) * 64, hd * 64:(hd + 1) * 64],
                               attb[j][cpar * 64:(cpar + 1) * 64, hf * 512 + (hh * 4 + ccn) * 64:hf * 512 + (hh * 4 + ccn + 1) * 64],
                               False, True, [f"Vtok{blk}", f"attb{j}"], [f"ps{bk}"], rg=cpar)
                    yield
                    act(osq[:, cs], psb[bk][:], AF.Square, [f"ps{bk}"], [kosq])
                    bk2 = nb(True)
                    mm(psb[bk2][:], bones_bf[:], osq[:, cs], True, True, ["bones", kosq], [f"ps{bk2}"])
                    yield
                    act(orst[:, 0:512], psb[bk2][:], AF.Sqrt, [f"ps{bk2}", "cst"], [korst], bias=cc("eps"), scale=1.0 / 64)
                    reserved.discard(bk2)
                    yield
                    recip(orst[:, 0:512], orst[:, 0:512], [korst], [korst])
                    yield
                    stt(t1[:, 0:512], psb[bk][:], cc(f"cng_{l}"), orst[:, 0:512], ALU.mult, ALU.mult, [f"ps{bk}", "cst", korst], [kt1])
                    reserved.discard(bk)
                    yield
                    tt(mix[4 + j][:, cs], t1[:, 0:512], gs[j][:, cs], ALU.mult, [kt1, kgs_[j]], [f"mix{4 + j}"])


            if stop == 'C':
                return []
            s_d = load_piece(l, 8)

            def genD(j, ti):
                xd, s2, s4, s8, s16, wsc = [T[k] for k in ti]
                kxd, k2, k4, k8, k16, kws = [f"T{k}" for k in ti]
                W = 15 + TT
                cp(xd[:, 0:15], st_xd[:, l * 2 + j, :], ["st_xd"], [kxd])
                for hf in range(NH):
                    bk = nb()
                    proj_fm(s_d, j * 128, bk, hf)
                    act(xd[:, 15 + hf * 512:15 + (hf + 1) * 512], psb[bk][:], AF.Identity, [f"ps{bk}"], [kxd])
                yield
                cp(st_xd[:, l * 2 + j, :], xd[:, TT:TT + 15], [kxd], ["st_xd"])
                tt(s2[:, 1:W], xd[:, 1:W], xd[:, 0:W - 1], ALU.add, [kxd], [k2])
                yield
                tt(s4[:, 3:W], s2[:, 3:W], s2[:, 1:W - 2], ALU.add, [k2], [k4])
                yield
                if j == 0:
                    lo, hi, klo, khi = s2, s4, k2, k4
                else:
                    tt(s8[:, 7:W], s4[:, 7:W], s4[:, 3:W - 4], ALU.add, [k4], [k8])
                    yield
                    tt(s16[:, 15:W], s8[:, 15:W], s8[:, 7:W - 8], ALU.add, [k8], [k16])
                    yield
                    lo, hi, klo, khi = s8, s16, k8, k16
                ts(wsc[0:64, 0:TT], lo[0:64, 15:W], cc("rw", j)[0:64], None, ALU.mult, None, [klo, "cst"], [kws])
                ts(wsc[64:128, 0:TT], hi[64:128, 15:W], cc("rw", j)[64:128], None, ALU.mult, None, [khi, "cst"], [kws])
                yield
                if first:
                    tt(wsc[:, 0:16], wsc[:, 0:16], cc("corr", j * 16, 16), ALU.mult, [kws, "cst"], [kws])
                    yield
                tt(pooled2[j][:], wsc[:, 0:TT], xd[:, 15:W], ALU.subtract, [kws, kxd], [f"pooled{j}"])
                yield
                for hf in range(NH):
                    cs = slice(hf * 512, (hf + 1) * 512)
                    bk = nb()
                    mm(psb[bk][:], bd_bf[:, l * 6 + 4 + j, :], pooled2[j][:, cs], True, True, ["bd", f"pooled{j}"], [f"ps{bk}"])
                    act(mix[6 + j][:, cs], psb[bk][:], AF.Identity, [f"ps{bk}", "cst"], [f"mix{6 + j}"], scale=cc(f"dsc_{l}", j))

            def chain(*gens):
                for g in gens:
                    yield from g

            def genCall():
                yield from rr(genC1(0, tiC[0]), genC1(1, tiC[1]))
                yield from genCmid()
                yield from rr(genC3(0, tiC[0]), genC3(1, tiC[1]))

            tiB = list(range(16, 23))
            if mixsel is None:
                interleave(genCall(), chain(genB(0, tiB), genB(1, tiB), genD(0, tiB[:6]), genD(1, tiB[:6])))
            else:
                if mixsel == 'B':
                    interleave(chain(genB(0, tiB), genB(1, tiB)))
                elif mixsel == 'D':
                    interleave(chain(genD(0, tiB[:6]), genD(1, tiB[:6])))
                elif mixsel == 'C1':
                    interleave(rr(genC1(0, tiC[0]), genC1(1, tiC[1])))
                elif mixsel == 'C2':
                    interleave(chain(rr(genC1(0, tiC[0]), genC1(1, tiC[1])), genCmid()))
                elif mixsel == 'C':
                    interleave(genCall())
                return []

            if dbg and l == 0 and i == dbg_tile and dbg_what == "mix":
                for c in range(8):
                    cp(T[11][:, 0:TT], mix[c][:], [f"mix{c}"], ["T11"])
                    P.op("sp", lambda e, c=c: e.dma_start(out=dbgT[c], in_=T[11][:, 0:TT]), reads=["T11"], dma_sem="dbg")

            if stop == 'mix':
                return []
            for m4 in range(4):
                s_o = load_piece(l, 9 + m4)
                wv = rview(s_o, 8, 256)
                for mm_ in range(2):
                    m = m4 * 2 + mm_
                    for hf in range(NH):
                        cs = slice(hf * 512, (hf + 1) * 512)
                        bk = nb()
                        for kc in range(8):
                            mm(psb[bk][:], wv[:, kc, mm_ * 128:(mm_ + 1) * 128], mix[kc][:, cs], kc == 0, kc == 7,
                               [f"ring{s_o}", f"mix{kc}"], [f"ps{bk}"])
                        tt(x[m][:, cs], x[m][:, cs], psb[bk][:], ALU.add, [f"x{m}", f"ps{bk}"], [f"x{m}"])
            if dbg and l == 0 and i == dbg_tile and dbg_what == "x1":
                for c in range(8):
                    P.op("sp", lambda e, c=c: e.dma_start(out=dbgT[c], in_=x[c][:]), reads=[f"x{c}"], dma_sem="dbg")

            if stop == 'wout':
                return []
            rmsnorm(f"g2_{l}")
            PF = 6
            fslot = {}
            for jj in range(min(PF, 22)):
                fslot[jj] = load_piece(l, 13 + jj)

            def genF(jj, ti):
                if jj + PF < 22:
                    fslot[jj + PF] = load_piece(l, 13 + jj + PF)
                s_w = fslot[jj]
                wv = rview(s_w, 8, 256)
                y = [T[ti[0]], T[ti[1]]]
                gl = T[ti[2]]
                ky = [f"T{ti[0]}", f"T{ti[1]}"]
                kgl = f"T{ti[2]}"
                chs = [jj, jj + 22]
                fw = lambda k, ch: cc(f"fcw_{l}", ch * 3 + k)
                bks = []
                for w_ in range(2):
                    bk = nb(True)
                    bks.append(bk)
                    for kc in range(8):
                        mm(psb[bk][:], wv[:, kc, w_ * 128:(w_ + 1) * 128], h[kc][:, 0:512], kc == 0, kc == 7,
                           [f"ring{s_w}", f"h{kc}"], [f"ps{bk}"])
                    act(y[w_][:, 0:512], psb[bk][:], AF.Identity, [f"ps{bk}", "cst"], [ky[w_]], bias=cc(f"fcb_{l}", chs[w_]), scale=fw(2, chs[w_]))
                    yield
                for k in (1, 0):
                    for w_ in range(2):
                        bk = bks[w_]
                        hal = st_hf[:, l * 44 + chs[w_], :]
                        khal = f"hf{l}_{chs[w_]}"
                        n_ = 2 - k
                        stt(y[w_][:, n_:512], psb[bk][:, 0:512 - n_], fw(k, chs[w_]), y[w_][:, n_:512], ALU.mult, ALU.add,
                            [f"ps{bk}", "cst", ky[w_]], [ky[w_]])
                        stt(y[w_][:, 0:n_], hal[:, k:2], fw(k, chs[w_]), y[w_][:, 0:n_], ALU.mult, ALU.add,
                            [khal, "cst", ky[w_]], [ky[w_]])
                    yield
                for w_ in range(2):
                    act(st_hf[:, l * 44 + chs[w_], :], psb[bks[w_]][:, 510:512], AF.Identity, [f"ps{bks[w_]}"], [f"hf{l}_{chs[w_]}"])
                    reserved.discard(bks[w_])
                act(gl[:, 0:TT], y[0][:, 0:TT], AF.Square, [ky[0]], [kgl])
                yield
                stt(gl[:, 0:TT], gl[:, 0:TT], GELU_K, y[0][:, 0:TT], ALU.add, ALU.mult, [kgl, ky[0]], [kgl])
                yield
                act(gl[:, 0:TT], gl[:, 0:TT], AF.Sigmoid, [kgl], [kgl], scale=GELU_C2)
                tt(y[0][:, 0:TT], y[0][:, 0:TT], y[1][:, 0:TT], ALU.mult, [ky[0], ky[1]], [ky[0]])
                yield
                tt(actb[jj][:], gl[:, 0:TT], y[0][:, 0:TT], ALU.mult, [kgl, ky[0]], [f"actb{jj}"])

            for jj in range(0, 22, 2):
                interleave(genF(jj, range(0, 3)), genF(jj + 1, range(3, 6)))
            for m in range(8):
                s_wa = load_piece(l, 40 + 2 * m)
                s_wb = load_piece(l, 40 + 2 * m + 1)
                wvs = [rview(s_wa, 11, 128), rview(s_wb, 11, 128)]
                sws = [s_wa, s_wb]
                for hf in range(NH):
                    cs = slice(hf * 512, (hf + 1) * 512)
                    bk = nb()
                    for kc in range(22):
                        mm(psb[bk][:], wvs[kc // 11][:, kc % 11, :], actb[kc][:, cs], kc == 0, kc == 21, [f"ring{sws[kc // 11]}", f"actb{kc}"], [f"ps{bk}"])
                    tt(x[m][:, cs], x[m][:, cs], psb[bk][:], ALU.add, [f"x{m}", f"ps{bk}"], [f"x{m}"])
            if dbg and l == 0 and i == dbg_tile and dbg_what == "x2":
                for c in range(8):
                    P.op("sp", lambda e, c=c: e.dma_start(out=dbgT[c], in_=x[c][:]), reads=[f"x{c}"], dma_sem="dbg")

            if stop == 'ffn':
                return []
            rmsnorm(f"g3_{l}")
            s_pe = None
            for m4 in range(4):
                s_g_ = load_piece(l, 35 + m4)
                if m4 == 0:
                    s_pe = load_piece(l, 39)
                wv = rview(s_g_, 8, 256)
                wpe = rview(s_pe, 2, 1024)
                for mm_ in range(2):
                    m = m4 * 2 + mm_
                    for hf in range(NH):
                        cs = slice(hf * 512, (hf + 1) * 512)
                        bk = nb()
                        for kc in range(8):
                            mm(psb[bk][:], wv[:, kc, mm_ * 128:(mm_ + 1) * 128], h[kc][:, cs], kc == 0, kc == 7, [f"ring{s_g_}", f"h{kc}"], [f"ps{bk}"])
                        ga, gb_ = 2 * (m % 4), 2 * (m % 4) + 1
                        act(T[ga][:, 0:512], psb[bk][:], AF.Sigmoid, [f"ps{bk}"], [f"T{ga}"])
                        bk = nb()
                        for kc in range(2):
                            mm(psb[bk][:], wpe[:, kc, m * 128:(m + 1) * 128], pbf[pa][kc][:, cs], kc == 0, kc == 1,
                               [f"ring{s_pe}", f"p{pa}_{kc}"], [f"ps{bk}"])
                        tt(T[gb_][:, 0:512], psb[bk][:], T[ga][:, 0:512], ALU.mult, [f"ps{bk}", f"T{ga}"], [f"T{gb_}"])
                        tt(x[m][:, cs], x[m][:, cs], T[gb_][:, 0:512], ALU.add, [f"x{m}", f"T{gb_}"], [f"x{m}"])

            toks = []
            if l == L - 1:
                for hf in range(NH):
                    cs = slice(hf * 512, (hf + 1) * 512)
                    bk = nb()
                    for c in range(8):
                        act(sqb2[c % 2][:, cs], x[c][:, cs], AF.Square, [f"x{c}"], [f"sqb{c % 2}"])
                        mm(psb[bk][:], ones_bf[:], sqb2[c % 2][:, cs], c == 0, c == 7, ["ones", f"sqb{c % 2}"], [f"ps{bk}"])
                    act(rstd[:, cs], psb[bk][:], AF.Sqrt, [f"ps{bk}", "cst"], ["rstd"], bias=cc("eps"), scale=1.0 / D)
                    recip(rstd[:, cs], rstd[:, cs], ["rstd"], ["rstd"])
                for c in range(8):
                    stt(x[c][:], x[c][:], cc("gf", c), rstd[:], ALU.mult, ALU.mult, [f"x{c}", "cst", "rstd"], [f"x{c}"])
                    toks.append(P.op("sp", lambda e, c=c: e.dma_start(out=outT[c, :, t0:t0 + TT], in_=x[c][:]), reads=[f"x{c}"], dma_sem=f"o{c}"))
            return toks

        dbg_tile, dbg_what = (dbg if isinstance(dbg, tuple) else (0, "mix"))
        out_toks = {}
        for i in range(NT):
            for l in range(L):
                if (i * L + l) % 2 == 0:
                    P.new_epoch()
                for tk in step(l, i):
                    out_toks[tk[0]] = tk
        if dbg:
            out_toks["dma_dbg"] = ("dma_dbg", P.cnt["dma_dbg"])
        if stop:
            out_toks = {}
        P.wait_tokens("sp", list(out_toks.values()))
        P.emit()
    return nc


_CACHE = {}


def kernel(**inputs):
    inp = {k: np.asarray(v) for k, v in inputs.items()}
    S = SEQ
    key = ("main", S)
    if key not in _CACHE:
        _CACHE[key] = build_program(S=S, TT=512, L=DEPTH)
    nc = _CACHE[key]
    in_maps = [prep_core_inputs(inp, b, S) for b in range(BATCH)]
    res = run_bass_kernel_spmd(nc, in_maps, core_ids=list(range(BATCH)))
    out = np.empty((BATCH, S, D), np.float32)
    for b in range(BATCH):
        out[b] = res.results[b]["outT"].reshape(D, S).T
    return out
```
